# Optimizing a Trainium2 kernel written in Bass

```python
import jax, jax.numpy as jnp
from jax import lax
import numpy as np

D_MODEL = 1024
BATCH = 16
SEQ = 2048
DEPTH = 4
DEC_BATCH = 32
DEC_SEQ = 16
PAST_LEN = 2048

CHUNK = 64
D_A = D_MODEL
D_B = D_MODEL
CONV_A_W = 3
CONV_B_W = 31
POOL_WINDOWS = (2, 4, 8, 16)
N_POOL_GROUPS = len(POOL_WINDOWS)
POOL_G = D_MODEL // N_POOL_GROUPS
POOL_PREV = max(POOL_WINDOWS) - 1
D_FF = 4 * D_MODEL
N_CONV_LAYERS = (DEPTH + 1) // 2
N_POOL_LAYERS = DEPTH // 2
D_IN_CONV = 3 * D_A + 2 * D_B
RMS_EPS = 1e-6
LN_EPS = 1e-5

kernel_name = "hybrid_streaming_conv_pool_encoder_step"


def _rmsnorm(x, g):
    x32 = x.astype(jnp.float32)
    y = x32 * lax.rsqrt(jnp.mean(x32 * x32, axis=-1, keepdims=True) + RMS_EPS) * g.astype(jnp.float32)
    return y.astype(x.dtype)


def _layernorm(x, g, b):
    x32 = x.astype(jnp.float32)
    mu = jnp.mean(x32, axis=-1, keepdims=True)
    var = jnp.mean(jnp.square(x32 - mu), axis=-1, keepdims=True)
    y = (x32 - mu) * lax.rsqrt(var + LN_EPS) * g.astype(jnp.float32) + b.astype(jnp.float32)
    return y.astype(x.dtype)


def _swiglu(u, w_gate, w_up, w_down):
    h = jax.nn.silu(u @ w_gate) * (u @ w_up)
    return h @ w_down


def _causal_dwconv(u, prev, w):
    u_ext = jnp.concatenate([prev.astype(u.dtype), u], axis=1)
    y = lax.conv_general_dilated(u_ext, w[:, None, :].astype(u.dtype), window_strides=(1,), padding='VALID',
                                 dimension_numbers=('NWC', 'WIO', 'NWC'), feature_group_count=u.shape[-1])
    return y, u_ext[:, -(w.shape[0] - 1):]


def _conv_mixer(u, prev_a, prev_b, w_in, conv_a_w, conv_b_w, ln_g, ln_b, w_out):
    z = u @ w_in
    h_a, b_a, c_a, v_b, g_b = jnp.split(z, [D_A, 2 * D_A, 3 * D_A, 3 * D_A + D_B], axis=-1)
    ya, new_a = _causal_dwconv(c_a * h_a, prev_a, conv_a_w)
    ya = b_a * ya
    yb, new_b = _causal_dwconv(v_b * jax.nn.sigmoid(g_b), prev_b, conv_b_w)
    yb = jax.nn.silu(_layernorm(yb, ln_g, ln_b))
    y = jnp.concatenate([ya, yb], axis=-1) @ w_out
    return y, new_a, new_b


def _pool_mixer(u, prev, start_pos, w_pool, scale):
    bsz, t_len, _ = u.shape
    u_ext = jnp.concatenate([prev.astype(u.dtype), u], axis=1)
    c = jnp.cumsum(u_ext.astype(jnp.float32), axis=1)
    c = jnp.pad(c, ((0, 0), (1, 0), (0, 0)))
    pos = (start_pos + jnp.arange(t_len)).astype(jnp.float32)
    u32 = u.astype(jnp.float32)
    outs = []
    for gi, win in enumerate(POOL_WINDOWS):
        sl = slice(gi * POOL_G, (gi + 1) * POOL_G)
        hi = c[:, POOL_PREV + 1:POOL_PREV + 1 + t_len, sl]
        lo = c[:, POOL_PREV + 1 - win:POOL_PREV + 1 - win + t_len, sl]
        cnt = jnp.minimum(jnp.float32(win), pos + 1.0)[None, :, None]
        outs.append((hi - lo) / cnt - u32[:, :, sl])
    p = jnp.stack(outs, axis=2)
    y = jnp.einsum('btgi,gio->btgo', p, w_pool.astype(jnp.float32)).reshape(bsz, t_len, D_MODEL)
    y = y * scale.astype(jnp.float32)
    return y.astype(u.dtype), u_ext[:, -POOL_PREV:]


def _trunk(x, st_a, st_b, st_p, start_pos, norm_g, final_norm_g, w_ffn_gate, w_ffn_up, w_ffn_down,
           w_in_conv, conv_a_w, conv_b_w, ln_b_g, ln_b_b, w_out_conv, w_pool, pool_scale):
    new_a, new_b, new_p = [], [], []
    for layer in range(DEPTH):
        x = x + 0.5 * _swiglu(_rmsnorm(x, norm_g[layer, 0]), w_ffn_gate[layer, 0], w_ffn_up[layer, 0], w_ffn_down[layer, 0])
        u = _rmsnorm(x, norm_g[layer, 1])
        i = layer // 2
        if layer % 2 == 0:
            y, sa, sb = _conv_mixer(u, st_a[i], st_b[i], w_in_conv[i], conv_a_w[i], conv_b_w[i],
                                    ln_b_g[i], ln_b_b[i], w_out_conv[i])
            new_a.append(sa)
            new_b.append(sb)
        else:
            y, sp = _pool_mixer(u, st_p[i], start_pos, w_pool[i], pool_scale[i])
            new_p.append(sp)
        x = x + y
        x = x + 0.5 * _swiglu(_rmsnorm(x, norm_g[layer, 2]), w_ffn_gate[layer, 1], w_ffn_up[layer, 1], w_ffn_down[layer, 1])
    return _rmsnorm(x, final_norm_g), jnp.stack(new_a), jnp.stack(new_b), jnp.stack(new_p)


def setup_inputs(seed: int = 0) -> dict:
    key = jax.random.key(seed)
    ks = jax.random.split(key, 20)
    nrm = lambda k, s, sc: jax.random.normal(k, s, jnp.float32) * sc
    return {
        "x_prompt": nrm(ks[0], (BATCH, SEQ, D_MODEL), 1.0),
        "x_sample": nrm(ks[1], (DEC_BATCH, DEC_SEQ, D_MODEL), 1.0),
        "state_conv_a": nrm(ks[2], (N_CONV_LAYERS, DEC_BATCH, CONV_A_W - 1, D_A), 1.0),
        "state_conv_b": nrm(ks[3], (N_CONV_LAYERS, DEC_BATCH, CONV_B_W - 1, D_B), 1.0),
        "state_pool": nrm(ks[4], (N_POOL_LAYERS, DEC_BATCH, POOL_PREV, D_MODEL), 1.0),
        "norm_g": 1.0 + nrm(ks[5], (DEPTH, 3, D_MODEL), 0.05),
        "final_norm_g": 1.0 + nrm(ks[6], (D_MODEL,), 0.05),
        "w_ffn_gate": nrm(ks[7], (DEPTH, 2, D_MODEL, D_FF), D_MODEL ** -0.5),
        "w_ffn_up": nrm(ks[8], (DEPTH, 2, D_MODEL, D_FF), D_MODEL ** -0.5),
        "w_ffn_down": nrm(ks[9], (DEPTH, 2, D_FF, D_MODEL), D_FF ** -0.5),
        "w_in_conv": nrm(ks[10], (N_CONV_LAYERS, D_MODEL, D_IN_CONV), D_MODEL ** -0.5),
        "conv_a_w": nrm(ks[11], (N_CONV_LAYERS, CONV_A_W, D_A), CONV_A_W ** -0.5),
        "conv_b_w": nrm(ks[12], (N_CONV_LAYERS, CONV_B_W, D_B), CONV_B_W ** -0.5),
        "ln_b_g": 1.0 + nrm(ks[13], (N_CONV_LAYERS, D_B), 0.05),
        "ln_b_b": nrm(ks[14], (N_CONV_LAYERS, D_B), 0.02),
        "w_out_conv": nrm(ks[15], (N_CONV_LAYERS, D_A + D_B, D_MODEL), (D_A + D_B) ** -0.5),
        "w_pool": nrm(ks[16], (N_POOL_LAYERS, N_POOL_GROUPS, POOL_G, POOL_G), POOL_G ** -0.5),
        "pool_scale": 0.5 + nrm(ks[17], (N_POOL_LAYERS, D_MODEL), 0.1),
    }


def reference(x_prompt, x_sample, state_conv_a, state_conv_b, state_pool, norm_g, final_norm_g,
              w_ffn_gate, w_ffn_up, w_ffn_down, w_in_conv, conv_a_w, conv_b_w, ln_b_g, ln_b_b,
              w_out_conv, w_pool, pool_scale):
    assert x_sample.shape[1] <= CHUNK
    weights = (norm_g, final_norm_g, w_ffn_gate, w_ffn_up, w_ffn_down, w_in_conv, conv_a_w, conv_b_w,
               ln_b_g, ln_b_b, w_out_conv, w_pool, pool_scale)
    b = x_prompt.shape[0]
    dt = x_prompt.dtype
    z_a = jnp.zeros((N_CONV_LAYERS, b, CONV_A_W - 1, D_A), dt)
    z_b = jnp.zeros((N_CONV_LAYERS, b, CONV_B_W - 1, D_B), dt)
    z_p = jnp.zeros((N_POOL_LAYERS, b, POOL_PREV, D_MODEL), dt)
    y_prompt, pa_a, pa_b, pa_p = _trunk(x_prompt, z_a, z_b, z_p, 0, *weights)
    y_sample, sa_a, sa_b, sa_p = _trunk(x_sample, state_conv_a, state_conv_b, state_pool, PAST_LEN, *weights)
    return (y_prompt, y_sample, pa_a, sa_a, pa_b, sa_b, pa_p, sa_p)
```

```python
import os
import numpy as np
import concourse.bass as bass
import concourse.mybir as mybir
from concourse.bass_utils import run_bass_kernel_spmd

F32 = mybir.dt.float32
BF16 = mybir.dt.bfloat16
ALU = mybir.AluOpType
AF = mybir.ActivationFunctionType

NCORES = 8
D = 1024
KC = 8
FF = 4096
DEPTH = 4
SEQ = 2048
NPT = 1024
NST = 16
N = NPT + NST
SUB = [(0, 512), (512, 512), (1024, 16)]
HIST = 30
SEGS = [(0, NPT, HIST), (NPT, NST, HIST + NPT + HIST)]
EXT = HIST + NPT + HIST + NST
RMS_EPS = 1e-6
LN_EPS = 1e-5
NSLOT = 5
SLOTW = 4096
EPOCH = 6000
KD = 10

V_NORM = 0
V_FINAL = 12
V_CAW = 13
V_CBW = 19
V_LNG = 81
V_LNB = 83
V_PSC = 85
NVEC = 87


class TL:
    def __init__(self, nc, name, step):
        self.nc, self.name, self.step = nc, name, step
        self.n = 0
        self.sems = []

    def semval(self, cnt):
        e = (cnt - 1) // EPOCH
        while len(self.sems) <= e:
            self.sems.append(self.nc.alloc_semaphore(name=f"{self.name}_{len(self.sems)}"))
        return self.sems[e], ((cnt - 1) % EPOCH + 1) * self.step


class Eng:
    def __init__(self, nc, name, is_pe=False, attach=True):
        self.name = name
        self.tl = TL(nc, "t" + name, 1)
        self.ops = []
        self.seen = {}
        self.is_pe = is_pe
        self.attach = attach

    def force_signal(self):
        last = self.ops[-1]
        if last[2] is None:
            self.tl.n += 1
            last[2] = (self.tl.semval(self.tl.n)[0], 1)

    def emit(self, fn, deps, signal, dma_tl=None, dma_n=1):
        waits = []
        for tl, cnt in deps.items():
            if tl is self.tl and self.is_pe:
                continue
            if self.seen.get(tl, 0) >= cnt:
                continue
            self.seen[tl] = cnt
            waits.append(tl.semval(cnt))
        inc = None
        if dma_tl is not None:
            dma_tl.n += dma_n
            assert dma_tl.n < EPOCH
            inc = (dma_tl.semval(dma_tl.n)[0], 16)
            tok = (dma_tl, dma_tl.n)
            self.ops.append([waits, fn, inc])
        else:
            if signal:
                self.tl.n += 1
                inc = (self.tl.semval(self.tl.n)[0], 1)
                tok = (self.tl, self.tl.n)
            else:
                tok = (self.tl, self.tl.n + 1)
            self.ops.append([waits, fn, inc])
        return tok

    def replay(self, e):
        for waits, fn, inc in self.ops:
            if self.attach and waits:
                for s, v in waits[:-1]:
                    e.wait_ge(s, v)
                ins = fn(e)
                ins._wait_ge(*waits[-1])
            else:
                for s, v in waits:
                    e.wait_ge(s, v)
                ins = fn(e)
            if inc is not None:
                if isinstance(ins, (list, tuple)):
                    for i_ in ins:
                        i_.then_inc(inc[0], inc[1])
                else:
                    ins.then_inc(inc[0], inc[1])


class Prog:
    def __init__(self, nc):
        self.nc = nc
        self.pe = Eng(nc, "pe", is_pe=True, attach=False)
        self.act = Eng(nc, "act")
        self.dve = Eng(nc, "dve")
        self.pool = Eng(nc, "pool", attach=False)
        self.sp = Eng(nc, "sp", attach=False)
        self.engs = {e.tl: e for e in (self.pe, self.act, self.dve, self.pool, self.sp)}
        self.w = {}
        self.r = {}

    def _deps(self, reads, writes, me=None):
        d = {}

        def add(tl, c):
            if me is not None and me.is_pe and tl is me.tl:
                return
            eng = self.engs.get(tl)
            if eng is not None and c > tl.n:
                eng.force_signal()
            if d.get(tl, 0) < c:
                d[tl] = c

        for k in reads:
            if k in self.w:
                add(*self.w[k])
        for k in writes:
            if k in self.w:
                add(*self.w[k])
            for tl, c in self.r.get(k, {}).items():
                add(tl, c)
        return d

    def _commit(self, tok, reads, writes):
        tl, c = tok
        for k in reads:
            rr = self.r.setdefault(k, {})
            if rr.get(tl, 0) < c:
                rr[tl] = c
        for k in writes:
            self.w[k] = tok
            self.r[k] = {}

    def op(self, eng, fn, reads=(), writes=(), signal=False, dma_tl=None, dma_n=1):
        d = self._deps(reads, writes, eng)
        tok = eng.emit(fn, d, signal, dma_tl, dma_n)
        self._commit(tok, reads, writes)
        return tok


def build_program(npass=4):
    nc = bass.Bass("TRN2", target_bir_lowering=False)
    P = Prog(nc)
    pe, act, dve, pool, sp = P.pe, P.act, P.dve, P.pool, P.sp

    def dram(name, shape, kind="ExternalInput"):
        return nc.dram_tensor(name, list(shape), F32, kind=kind).ap()

    xp = dram("xp", [2, SEQ, D])
    xs = dram("xs", [4, NST, D])
    sta = dram("sta", [2, 4, 2, D])
    stb = dram("stb", [2, 4, 30, D])
    stp = dram("stp", [2, 4, 15, D])
    vecs = dram("vecs", [128, NVEC * 8])
    cst = dram("cst", [128, 128 + 128 + 64 + 2])
    wg = dram("wg", [DEPTH, 2, D, FF])
    wu = dram("wu", [DEPTH, 2, D, FF])
    wd = dram("wd", [DEPTH, 2, FF, D])
    win = dram("win", [2, D, 5 * D])
    wout = dram("wout", [2, 2 * D, D])
    wpool = dram("wpool", [2, 4, 256, 256])
    yp = dram("yp", [2, SEQ, D], "ExternalOutput")
    ys = dram("ys", [4, NST, D], "ExternalOutput")
    oap = dram("oap", [2, 2, 2, D], "ExternalOutput")
    oas = dram("oas", [2, 4, 2, D], "ExternalOutput")
    obp = dram("obp", [2, 2, 30, D], "ExternalOutput")
    obs = dram("obs", [2, 4, 30, D], "ExternalOutput")
    opp = dram("opp", [2, 2, 15, D], "ExternalOutput")
    ops_ = dram("ops", [2, 4, 15, D], "ExternalOutput")

    def sb(name, shape, dt):
        return nc.alloc_sbuf_tensor(name, list(shape), dt)

    X = sb("X", [128, KC, N], F32)
    U = sb("U", [128, KC, N], BF16)
    H = sb("H", [128, KC, N], BF16)
    B = sb("B", [128, KC, EXT], F32)
    WS = [sb(f"W{i}", [128, SLOTW], BF16) for i in range(NSLOT)]
    WTL = [TL(nc, f"wl{i}", 16) for i in range(NSLOT)]
    E32 = [sb(f"E32_{i}", [128, EXT], F32) for i in range(2)]
    E16 = [sb(f"E16_{i}", [128, EXT], BF16) for i in range(2)]
    S1 = [sb(f"S1_{i}", [128, N], F32) for i in range(2)]
    S2 = [sb(f"S2_{i}", [128, N], BF16) for i in range(2)]
    DG = sb("DG", [128, 31, 128], BF16)
    V = sb("V", [128, NVEC * 8], F32)
    CST = sb("CST", [128, 322], F32)
    IDB = sb("IDB", [128, 128], BF16)
    ONB = sb("ONB", [128, 128], BF16)
    IOB = [sb(f"IOB{i}", [128, D], F32) for i in range(2)]
    IOTL = [TL(nc, f"iol{i}", 16) for i in range(2)]
    OSTL = [TL(nc, f"ios{i}", 16) for i in range(2)]
    CTL = TL(nc, "cl", 16)
    R1 = sb("R1", [128, 512], F32)
    R2 = sb("R2", [128, 512], F32)
    RS = sb("RS", [128, 512], F32)
    RS3 = [RS, sb("RSb", [128, 512], F32), sb("RSc", [128, 16], F32)]
    MU = sb("MU", [128, 512], F32)
    MU3 = [MU, sb("MUb", [128, 512], F32), sb("MUc", [128, 16], F32)]
    SG = [sb(f"SG{i}", [128, 512], F32) for i in range(2)]
    HA = [sb(f"HA{i}", [128, KC, 2, 2], F32) for i in range(2)]
    HB = [sb(f"HB{i}", [128, KC, 2, 30], F32) for i in range(2)]
    HP = [sb(f"HP{i}", [128, KC, 2, 15], F32) for i in range(2)]
    PS = [nc.alloc_psum_tensor(f"ps{i}", [128, 512], F32) for i in range(8)]

    IDENT = CST[:, 0:128]
    ONES32 = CST[:, 128:256]
    INVC = CST[:, 256:320]
    EPS_RMS = CST[:, 320:321]
    EPS_LN = CST[:, 321:322]

    st = {"bank": 0, "slot": 0, "io": 0, "sg": 0}

    def nbank():
        b = st["bank"]
        st["bank"] = (b + 1) % 8
        return b

    def vcol(vec, c):
        col = vec * 8 + c
        return V[:, col:col + 1]

    def xkeys(name, cs, ss):
        return [(name, c, s) for c in cs for s in ss]

    ALLS = range(len(SUB))

    def mm(out, lhsT, rhs, start, stop, reads, writes, signal=False):
        return P.op(pe, lambda e: e.matmul(out, lhsT, rhs, start=start, stop=stop), reads, writes, signal)

    def tr(out, in_, ident, reads, writes, signal=False):
        return P.op(pe, lambda e: e.transpose(out, in_, ident), reads, writes, signal)

    def load_slab(dst_fn, src):
        k = st["slot"] % NSLOT
        st["slot"] += 1
        if isinstance(src, list):
            pairs = [(dst_fn(WS[k], j), sj) for j, sj in enumerate(src)]
        else:
            pairs = [(dst_fn(WS[k]), src)]
        P.op(pool, lambda e: [e.dma_start(out=d_, in_=s_) for d_, s_ in pairs], (), [("W", k)], dma_tl=WTL[k],
             dma_n=len(pairs))
        return k

    def io_buf():
        b = st["io"] % 2
        st["io"] += 1
        return b

    P.op(sp, lambda e: e.dma_start(out=V[:, :], in_=vecs[:, :]), (), ["V"], dma_tl=CTL)
    P.op(sp, lambda e: e.dma_start(out=CST[:, :], in_=cst[:, :]), (), ["CST"], dma_tl=TL(nc, "cl2", 16))
    P.op(act, lambda e: e.copy(out=IDB[:, :], in_=IDENT), ["CST"], ["IDB"])
    P.op(dve, lambda e: e.memset(ONB[:, :], 1.0), (), ["ONB"])
    for i in range(2):
        P.op(dve, lambda e, i=i: e.memset(E32[i][:, :], 0.0), (), [("E32", i)])
    for c in range(KC):
        P.op(dve, lambda e, c=c: e.memset(B[:, c, :], 0.0), (), [("B", c)])

    def make_norm(vec, dst_kind, early_apply=True, defer_late=False):
        banks = {}

        def sq(s):
            t0, n = SUB[s]
            P.op(act, lambda e: e.activation(out=U[:, :, t0:t0 + n], in_=X[:, :, t0:t0 + n], func=AF.Square),
                 xkeys("X", range(KC), [s]), xkeys("U", range(KC), [s]), signal=True)

        def stats(s):
            t0, n = SUB[s]
            b = nbank()
            banks[s] = b
            for c in range(KC):
                mm(PS[b][:, 0:n], ONB[:, :], U[:, c, t0:t0 + n], c == 0, c == KC - 1, [("U", c, s), "ONB"], [("ps", b)],
                   signal=(c == KC - 1))

        def chain(s):
            t0, n = SUB[s]
            b = banks[s]
            P.op(act, lambda e: e.activation(out=R1[:, 0:n], in_=PS[b][:, 0:n], func=AF.Ln, bias=EPS_RMS, scale=1.0 / D),
                 [("ps", b), "CST"], ["R1"], signal=True)
            P.op(act, lambda e: e.activation(out=RS3[s][:, 0:n], in_=R1[:, 0:n], func=AF.Exp, scale=-0.5),
                 ["R1"], [("RS", s)], signal=True)

        def apply(s):
            t0, n = SUB[s]
            for c in range(KC):
                if dst_kind == "U":
                    out, wk = U[:, c, t0:t0 + n], ("U", c, s)
                elif dst_kind == "Y":
                    out, wk = B[:, c, t0:t0 + n], ("B", c)
                else:
                    seg = 0 if t0 < NPT else 1
                    e0 = SEGS[seg][2] + (t0 - SEGS[seg][0])
                    out, wk = B[:, c, e0:e0 + n], ("B", c)
                rk = [("X", c, s), ("RS", s), "V"] + ([] if dst_kind == "U" else [("U", c, s)])
                P.op(dve, lambda e, c=c, out=out: e.scalar_tensor_tensor(
                    out=out, in0=X[:, c, t0:t0 + n], scalar=vcol(vec, c), in1=RS3[s][:, 0:n],
                    op0=ALU.mult, op1=ALU.mult), rk, [wk], signal=True)

        last = len(SUB) - 1
        flags = {"mid": False}

        def mid_sub(s):
            if s == 1 and not flags["mid"]:
                flags["mid"] = True
                stats(0)
                chain(0)
                if early_apply:
                    apply(0)

        def tail_part():
            if not early_apply:
                apply(0)
            for t in range(1, last + 1):
                stats(t)
                chain(t)
            for t in range(1, last + 1):
                apply(t)

        def late():
            if flags.get("late"):
                flags["late"] = False
                tail_part()

        def after_sub(s):
            sq(s)
            if s == 1:
                mid_sub(1)
            if s == last:
                if defer_late and early_apply:
                    flags["late"] = True
                else:
                    tail_part()

        after_sub.late = late
        after_sub.mid = mid_sub
        return after_sub

    def ffn(l, f, tail_cb, own=None):
        NG = FF // 512

        def gateup(j, hook=None):
            kg = load_slab(lambda w: w[:, 0:4096].rearrange("p (k n) -> p k n", k=KC),
                           wg[l, f, :, j * 512:(j + 1) * 512].rearrange("(k p) n -> p k n", p=128))
            ku = load_slab(lambda w: w[:, 0:4096].rearrange("p (k n) -> p k n", k=KC),
                           wu[l, f, :, j * 512:(j + 1) * 512].rearrange("(k p) n -> p k n", p=128))
            hb = (j % 2) * 4
            for s, (t0, n) in enumerate(SUB):
                for m in range(4):
                    bg = nbank()
                    bu = nbank()
                    for k in range(KC):
                        mm(PS[bg][:, 0:n], WS[kg][:, k * 512 + m * 128:k * 512 + (m + 1) * 128], U[:, k, t0:t0 + n],
                           k == 0, k == KC - 1, [("W", kg), ("U", k, s)], [("ps", bg)], signal=(k == KC - 1))
                    for k in range(KC):
                        mm(PS[bu][:, 0:n], WS[ku][:, k * 512 + m * 128:k * 512 + (m + 1) * 128], U[:, k, t0:t0 + n],
                           k == 0, k == KC - 1, [("W", ku), ("U", k, s)], [("ps", bu)], signal=(k == KC - 1))
                    g = st["sg"] % 2
                    st["sg"] += 1
                    P.op(act, lambda e, bg=bg, g=g, n=n: e.activation(out=SG[g][:, 0:n], in_=PS[bg][:, 0:n], func=AF.Silu),
                         [("ps", bg)], [("SG", g)], signal=True)
                    P.op(dve, lambda e, bu=bu, g=g, n=n, t0=t0, c=hb + m: e.tensor_tensor(
                        out=H[:, c, t0:t0 + n], in0=PS[bu][:, 0:n], in1=SG[g][:, 0:n], op=ALU.mult),
                        [("ps", bu), ("SG", g)], [("H", hb + m, s)], signal=True)
                    if hook is not None and s == 0 and m == 0:
                        hook()

        def load_wd(j):
            return load_slab(lambda w: w[:, 0:4096].rearrange("p (k n) -> p k n", k=4),
                             wd[l, f, j * 512:(j + 1) * 512, :].rearrange("(k p) n -> p k n", p=128))

        def down_sub(j, kd, s):
            hb = (j % 2) * 4
            t0, n = SUB[s]
            for mo in range(KC):
                b = nbank()
                for kk in range(4):
                    mm(PS[b][:, 0:n], WS[kd][:, kk * 1024 + mo * 128:kk * 1024 + (mo + 1) * 128],
                       H[:, hb + kk, t0:t0 + n], kk == 0, kk == 3, [("W", kd), ("H", hb + kk, s)], [("ps", b)],
                       signal=(kk == 3))
                P.op(dve, lambda e, b=b, mo=mo: e.scalar_tensor_tensor(
                    out=X[:, mo, t0:t0 + n], in0=PS[b][:, 0:n], scalar=0.5, in1=X[:, mo, t0:t0 + n],
                    op0=ALU.mult, op1=ALU.add), [("ps", b), ("X", mo, s)], [("X", mo, s)], signal=True)

        def down(j):
            kd = load_wd(j)
            for s in range(len(SUB)):
                down_sub(j, kd, s)

        gateup(0, hook=(own.late if own is not None else None))
        for j in range(1, NG):
            gateup(j)
            if j < NG - 1:
                down(j - 1)
        kda = load_wd(NG - 2)
        kdb = load_wd(NG - 1)
        for s in range(len(SUB)):
            down_sub(NG - 2, kda, s)
            if tail_cb is not None and s >= 1:
                tail_cb.mid(s)
            down_sub(NG - 1, kdb, s)
            if tail_cb is not None:
                tail_cb(s)

    def seg_of_sub(s):
        t0, n = SUB[s]
        seg = 0 if t0 < NPT else 1
        return seg, SEGS[seg][2] + (t0 - SEGS[seg][0])

    def conv_mixer(i, first_half, tail_cb, own=None):
        l = 2 * i

        def inproj(c, part, hook=None):
            q = c % 2
            if part == "A":
                wv5 = win[i].rearrange("(k p) (j c n) -> p k j c n", p=128, j=5, c=8)
                kw = load_slab(lambda w, j: w[:, 0:3072].rearrange("p (k j n) -> p k j n", k=KC, j=3)[:, :, j, :],
                               [wv5[:, :, j, c, :] for j in range(3)])
                nj = 3
            else:
                wv5 = win[i].rearrange("(k p) (j c n) -> p k j c n", p=128, j=5, c=8)
                kw = load_slab(lambda w, j: w[:, 0:2048].rearrange("p (k j n) -> p k j n", k=KC, j=2)[:, :, j, :],
                               [wv5[:, :, 3 + j, c, :] for j in range(2)])
                nj = 2
            for s, (t0, n) in enumerate(SUB):
                seg, e0 = seg_of_sub(s)
                banks = [nbank() for _ in range(nj)]
                for j in range(nj):
                    for k in range(KC):
                        mm(PS[banks[j]][:, 0:n], WS[kw][:, (k * nj + j) * 128:(k * nj + j + 1) * 128], U[:, k, t0:t0 + n],
                           k == 0, k == KC - 1, [("W", kw), ("U", k, s)], [("ps", banks[j])], signal=(k == KC - 1))
                if part == "A":
                    bh, bb, bc = banks
                    P.op(act, lambda e, bh=bh, q=q, t0=t0, n=n: e.copy(out=S1[q][:, t0:t0 + n], in_=PS[bh][:, 0:n]),
                         [("ps", bh)], [("S1", q)], signal=True)
                    P.op(dve, lambda e, bc=bc, q=q, t0=t0, n=n, e0=e0: e.tensor_tensor(
                        out=E32[q][:, e0:e0 + n], in0=PS[bc][:, 0:n], in1=S1[q][:, t0:t0 + n], op=ALU.mult),
                        [("ps", bc), ("S1", q)], [("E32", q)], signal=True)
                    P.op(act, lambda e, bb=bb, q=q, t0=t0, n=n: e.copy(out=S2[q][:, t0:t0 + n], in_=PS[bb][:, 0:n]),
                         [("ps", bb)], [("S2", q)], signal=True)
                    if hook is not None and s == 0:
                        hook()
                else:
                    bv, bgt = banks
                    P.op(act, lambda e, bgt=bgt, q=q, t0=t0, n=n: e.activation(out=S1[q][:, t0:t0 + n], in_=PS[bgt][:, 0:n],
                                                                               func=AF.Sigmoid),
                         [("ps", bgt)], [("S1", q)], signal=True)
                    P.op(dve, lambda e, bv=bv, q=q, t0=t0, n=n, e0=e0: e.tensor_tensor(
                        out=E32[q][:, e0:e0 + n], in0=PS[bv][:, 0:n], in1=S1[q][:, t0:t0 + n], op=ALU.mult),
                        [("ps", bv), ("S1", q)], [("E32", q)], signal=True)
            hw = 2 if part == "A" else 30
            HH = HA[i] if part == "A" else HB[i]
            hname = "HA" if part == "A" else "HB"
            for seg, (toff, ln, eoff) in enumerate(SEGS):
                if seg == 0 and first_half:
                    P.op(dve, lambda e, q=q, eoff=eoff, hw=hw: e.memset(E32[q][:, eoff - hw:eoff], 0.0),
                         (), [("E32", q)])
                else:
                    P.op(act, lambda e, q=q, eoff=eoff, hw=hw, seg=seg, HH=HH, c=c: e.copy(
                        out=E32[q][:, eoff - hw:eoff], in_=HH[:, c, seg, :]), [(hname, i, seg, c)], [("E32", q)])
            P.op(act, lambda e, q=q: e.copy(out=E16[q][:, :], in_=E32[q][:, :]), [("E32", q)], [("E16", q)], signal=True)
            for seg, (toff, ln, eoff) in enumerate(SEGS):
                P.op(act, lambda e, q=q, eoff=eoff, ln=ln, hw=hw, seg=seg, HH=HH, c=c: e.copy(
                    out=HH[:, c, seg, :], in_=E32[q][:, eoff + ln - hw:eoff + ln]), [("E32", q)], [(hname, i, seg, c)])
            if part == "B":
                dve_taps(c)

        def build_dg(c, part):
            ntap = 3 if part == "A" else 31
            wv = (V_CAW + i * 3) if part == "A" else (V_CBW + i * 31)
            for k in range(0 if part == "A" else KD, ntap):
                P.op(act, lambda e, k=k, c=c, wv=wv: e.activation(out=DG[:, k, :], in_=IDB[:, :], func=AF.Copy,
                                                                   scale=vcol(wv + k, c)),
                     ["IDB", "V"], [("DG", k)], signal=(k == ntap - 1))

        def conv(c, part):
            q = c % 2
            ntap = 3 if part == "A" else 31
            k0 = 0 if part == "A" else KD
            for s, (t0, n) in enumerate(SUB):
                seg, e0 = seg_of_sub(s)
                b = nbank()
                for k in range(k0, ntap):
                    off = e0 - (ntap - 1) + k
                    mm(PS[b][:, 0:n], DG[:, k, :], E16[q][:, off:off + n], k == k0, k == ntap - 1,
                       [("DG", k), ("E16", q)], [("ps", b)], signal=(k == ntap - 1))
                if part == "A":
                    P.op(dve, lambda e, b=b, q=q, c=c, t0=t0, n=n: e.tensor_tensor(
                        out=H[:, c, t0:t0 + n], in0=PS[b][:, 0:n], in1=S2[q][:, t0:t0 + n], op=ALU.mult),
                        [("ps", b), ("S2", q)], [("H", c, s)], signal=True)
                else:
                    P.op(dve, lambda e, b=b, c=c, e0=e0, n=n: e.tensor_tensor(
                        out=B[:, c, e0:e0 + n], in0=PS[b][:, 0:n], in1=B[:, c, e0:e0 + n], op=ALU.add),
                        [("ps", b), ("B", c)], [("B", c)], signal=True)

        def dve_taps(c):
            q = c % 2
            wv = V_CBW + i * 31
            w_ = EXT - HIST
            for k in range(KD):
                if k == 0:
                    P.op(dve, lambda e, k=k: e.tensor_scalar(out=B[:, c, HIST:EXT], in0=E32[q][:, k:k + w_],
                                                            scalar1=vcol(wv + k, c), scalar2=None, op0=ALU.mult),
                         [("E32", q), "V"], [("B", c)], signal=True)
                else:
                    P.op(dve, lambda e, k=k: e.scalar_tensor_tensor(out=B[:, c, HIST:EXT], in0=E32[q][:, k:k + w_],
                                                                     scalar=vcol(wv + k, c), in1=B[:, c, HIST:EXT],
                                                                     op0=ALU.mult, op1=ALU.add),
                         [("E32", q), "V", ("B", c)], [("B", c)], signal=True)

        for part in ("A", "B"):
            inproj(0, part, hook=(own.late if (own is not None and part == "A") else None))
            build_dg(0, part)
            for c in range(1, KC):
                inproj(c, part)
                conv(c - 1, part)
                build_dg(c, part)
            conv(KC - 1, part)

        def wout_part(s, part, kos, mid=None):
            t0, n = SUB[s]
            src, sname = (H, "H") if part == "A" else (U, "U")
            for mo in range(KC):
                if mo == KC // 2 and mid is not None:
                    mid(s)
                b = nbank()
                for kc in range(8):
                    ko = kos[kc // 4]
                    kk = kc % 4
                    mm(PS[b][:, 0:n], WS[ko][:, kk * 1024 + mo * 128:kk * 1024 + (mo + 1) * 128], src[:, kc, t0:t0 + n],
                       kc == 0, kc == 7, [("W", ko), (sname, kc, s)], [("ps", b)], signal=(kc == 7))
                P.op(dve, lambda e, b=b, mo=mo: e.tensor_tensor(
                    out=X[:, mo, t0:t0 + n], in0=PS[b][:, 0:n], in1=X[:, mo, t0:t0 + n], op=ALU.add),
                    [("ps", b), ("X", mo, s)], [("X", mo, s)], signal=True)

        def load_wout(qq):
            return load_slab(lambda w: w[:, 0:4096].rearrange("p (k n) -> p k n", k=4),
                             wout[i, qq * 512:(qq + 1) * 512, :].rearrange("(k p) n -> p k n", p=128))

        kosA = [load_wout(0), load_wout(1)]
        for s, (t0, n) in enumerate(SUB):
            e0 = seg_of_sub(s)[1]
            P.op(act, lambda e, t0=t0, n=n, e0=e0: e.activation(out=U[:, :, t0:t0 + n], in_=B[:, :, e0:e0 + n], func=AF.Square),
                 [("B", c) for c in range(KC)], xkeys("U", range(KC), [s]), signal=True)
        for s, (t0, n) in enumerate(SUB):
            e0 = seg_of_sub(s)[1]
            bs = nbank()
            bq = nbank()
            for c in range(KC):
                mm(PS[bs][:, 0:n], ONES32, B[:, c, e0:e0 + n], c == 0, c == KC - 1, [("B", c), "CST"], [("ps", bs)],
                   signal=(c == KC - 1))
            for c in range(KC):
                mm(PS[bq][:, 0:n], ONB[:, :], U[:, c, t0:t0 + n], c == 0, c == KC - 1, [("U", c, s), "ONB"], [("ps", bq)],
                   signal=(c == KC - 1))
            mu, rs = MU3[s], RS3[s]
            P.op(act, lambda e, bs=bs, n=n, mu=mu: e.copy(out=mu[:, 0:n], in_=PS[bs][:, 0:n]), [("ps", bs)], [("MU", s)], signal=True)
            P.op(dve, lambda e, n=n, mu=mu: e.tensor_tensor(out=R1[:, 0:n], in0=mu[:, 0:n], in1=mu[:, 0:n], op=ALU.mult),
                 [("MU", s)], ["R1"], signal=True)
            P.op(dve, lambda e, bq=bq, n=n: e.scalar_tensor_tensor(out=R2[:, 0:n], in0=PS[bq][:, 0:n], scalar=1.0 / D,
                                                                      in1=R1[:, 0:n], op0=ALU.mult, op1=ALU.subtract),
                 [("ps", bq), "R1"], ["R2"], signal=True)
            P.op(dve, lambda e, n=n: e.tensor_scalar(out=R2[:, 0:n], in0=R2[:, 0:n], scalar1=0.0, scalar2=None, op0=ALU.max),
                 ["R2"], ["R2"], signal=True)
            P.op(act, lambda e, n=n: e.activation(out=R1[:, 0:n], in_=R2[:, 0:n], func=AF.Ln, bias=EPS_LN, scale=1.0),
                 ["R2", "CST"], ["R1"], signal=True)
            P.op(act, lambda e, n=n, rs=rs: e.activation(out=rs[:, 0:n], in_=R1[:, 0:n], func=AF.Exp, scale=-0.5),
                 ["R1"], [("RS", s)], signal=True)
            for c in range(KC):
                g = st["sg"] % 2
                st["sg"] += 1
                P.op(dve, lambda e, g=g, c=c, e0=e0, n=n, mu=mu: e.tensor_tensor(out=SG[g][:, 0:n], in0=B[:, c, e0:e0 + n],
                                                                                 in1=mu[:, 0:n], op=ALU.subtract),
                     [("B", c), ("MU", s)], [("SG", g)], signal=True)
                P.op(dve, lambda e, g=g, n=n, rs=rs: e.tensor_tensor(out=SG[g][:, 0:n], in0=SG[g][:, 0:n], in1=rs[:, 0:n], op=ALU.mult),
                     [("SG", g), ("RS", s)], [("SG", g)], signal=True)
                P.op(act, lambda e, g=g, c=c, t0=t0, n=n: e.activation(
                    out=U[:, c, t0:t0 + n], in_=SG[g][:, 0:n], func=AF.Silu, bias=vcol(V_LNB + i, c), scale=vcol(V_LNG + i, c)),
                    [("SG", g), "V"], [("U", c, s)], signal=True)
            wout_part(s, "A", kosA)
        kosB = [load_wout(2), load_wout(3)]
        for s in range(len(SUB)):
            wout_part(s, "B", kosB, mid=(tail_cb.mid if (tail_cb is not None and s >= 1) else None))
            if tail_cb is not None:
                tail_cb(s)

    def pool_mixer(i, first_half, tail_cb):
        l = 2 * i + 1
        wp4 = wpool[i].rearrange("g (k p) o -> p g k o", p=128)
        kw = load_slab(lambda w, g: w[:, 0:2048].rearrange("p (g k o) -> p g k o", g=4, k=2)[:, g, :, :],
                       [wp4[:, g, :, :] for g in range(4)])
        for c in range(KC):
            for seg, (toff, ln, eoff) in enumerate(SEGS):
                if seg == 0 and first_half:
                    P.op(dve, lambda e, c=c, eoff=eoff: e.memset(B[:, c, eoff - 15:eoff], 0.0), (), [("B", c)])
                else:
                    P.op(act, lambda e, c=c, eoff=eoff, seg=seg: e.copy(out=B[:, c, eoff - 15:eoff], in_=HP[i][:, c, seg, :]),
                         [("HP", i, seg, c)], [("B", c)])
            for seg, (toff, ln, eoff) in enumerate(SEGS):
                P.op(act, lambda e, c=c, eoff=eoff, ln=ln, seg=seg: e.copy(out=HP[i][:, c, seg, :],
                                                                                  in_=B[:, c, eoff + ln - 15:eoff + ln]),
                     [("B", c)], [("HP", i, seg, c)])
            g = c // 2
            win_ = 2 << g
            src = B[:, c, :]
            srck = ("B", c)
            sh = 1
            q = 0
            lo = 15
            for step in range(g + 1):
                dst = E32[q]
                P.op(dve, lambda e, dst=dst, src=src, sh=sh, lo=lo: e.tensor_tensor(
                    out=dst[:, lo:EXT], in0=src[:, lo:EXT], in1=src[:, lo - sh:EXT - sh], op=ALU.add),
                    [srck], [("E32", q)], signal=True)
                src = dst
                srck = ("E32", q)
                sh *= 2
                q ^= 1
            for seg, (toff, ln, eoff) in enumerate(SEGS):
                P.op(dve, lambda e, src=src, c=c, toff=toff, ln=ln, eoff=eoff, win_=win_: e.scalar_tensor_tensor(
                    out=U[:, c, toff:toff + ln], in0=src[:, eoff:eoff + ln], scalar=1.0 / win_, in1=B[:, c, eoff:eoff + ln],
                    op0=ALU.mult, op1=ALU.subtract), [srck, ("B", c)], xkeys("U", [c], ALLS), signal=True)
                if seg == 0 and first_half:
                    P.op(dve, lambda e, src=src, eoff=eoff, g=g: e.tensor_tensor(
                        out=R1[:, 0:16], in0=src[:, eoff:eoff + 16], in1=INVC[:, g * 16:(g + 1) * 16], op=ALU.mult),
                        [srck, "CST"], ["R1"])
                    P.op(dve, lambda e, c=c, toff=toff, eoff=eoff: e.tensor_tensor(
                        out=U[:, c, toff:toff + 16], in0=R1[:, 0:16], in1=B[:, c, eoff:eoff + 16], op=ALU.subtract),
                        ["R1", ("B", c)], xkeys("U", [c], ALLS), signal=True)
        for s, (t0, n) in enumerate(SUB):
            for g in range(4):
                for mh in range(2):
                    mo = 2 * g + mh
                    b = nbank()
                    for kc in range(2):
                        base = (g * 2 + kc) * 256 + mh * 128
                        mm(PS[b][:, 0:n], WS[kw][:, base:base + 128], U[:, 2 * g + kc, t0:t0 + n], kc == 0, kc == 1,
                           [("W", kw), ("U", 2 * g + kc, s)], [("ps", b)], signal=(kc == 1))
                    P.op(dve, lambda e, b=b, mo=mo, t0=t0, n=n: e.scalar_tensor_tensor(
                        out=X[:, mo, t0:t0 + n], in0=PS[b][:, 0:n], scalar=vcol(V_PSC + i, mo), in1=X[:, mo, t0:t0 + n],
                        op0=ALU.mult, op1=ALU.add), [("ps", b), ("X", mo, s), "V"], [("X", mo, s)], signal=True)
            if tail_cb is not None:
                tail_cb(s)

    def load_rows_fm(src_rows, r, dst_fn, dst_keys):
        ib = io_buf()
        P.op(sp, lambda e: e.dma_start(out=IOB[ib][0:r, :], in_=src_rows), (), [("IOB", ib)], dma_tl=IOTL[ib])
        for half in range(2):
            b = nbank()
            for cc in range(4):
                c = half * 4 + cc
                tr(PS[b][:, cc * r:(cc + 1) * r], IOB[ib][0:r, c * 128:(c + 1) * 128], IDENT[0:r, 0:r],
                   [("IOB", ib), "CST"], [("ps", b)], signal=(cc == 3))
            eng = act if half == 0 else dve
            src = PS[b][:, 0:4 * r].rearrange("p (c t) -> p c t", c=4)
            if eng is act:
                P.op(act, lambda e, src=src, half=half: e.copy(out=dst_fn(half), in_=src), [("ps", b)], dst_keys, signal=True)
            else:
                P.op(dve, lambda e, src=src, half=half: e.tensor_copy(out=dst_fn(half), in_=src), [("ps", b)], dst_keys, signal=True)

    def store_rows_tm(src_fn, src_keys, r, dst_rows, out_tls):
        ib = io_buf()
        for half in range(2):
            b = nbank()
            for cc in range(4):
                c = half * 4 + cc
                tr(PS[b][0:r, cc * 128:(cc + 1) * 128], src_fn(c), IDENT, src_keys + ["CST"], [("ps", b)], signal=(cc == 3))
            if half == 0:
                P.op(act, lambda e, b=b: e.copy(out=IOB[ib][0:r, 0:512], in_=PS[b][0:r, 0:512]), [("ps", b)], [("IOB", ib)], signal=True)
            else:
                P.op(dve, lambda e, b=b: e.tensor_copy(out=IOB[ib][0:r, 512:1024], in_=PS[b][0:r, 0:512]), [("ps", b)], [("IOB", ib)], signal=True)
        tok = P.op(sp, lambda e: e.dma_start(out=dst_rows, in_=IOB[ib][0:r, :]), [("IOB", ib)], (), dma_tl=OSTL[ib])
        return tok

    for p in range(npass):
        seq, half = p // 2, p % 2
        first_half = (half == 0)
        sseq = p
        phases = []
        for l in range(DEPTH):
            phases.append((V_NORM + l * 3 + 0, "U", lambda cb, own, l=l: ffn(l, 0, cb, own)))
            if l % 2 == 0:
                phases.append((V_NORM + l * 3 + 1, "U", lambda cb, own, l=l: conv_mixer(l // 2, first_half, cb, own)))
            else:
                phases.append((V_NORM + l * 3 + 1, "B", lambda cb, own, l=l: pool_mixer(l // 2, first_half, cb)))
            phases.append((V_NORM + l * 3 + 2, "U", lambda cb, own, l=l: ffn(l, 1, cb, own)))
        after_pool = [False] + [kind == "B" for _, kind, _ in phases]
        norms = [make_norm(v, k, early_apply=not after_pool[j], defer_late=(k == "U")) for j, (v, k, _) in enumerate(phases)]
        norms.append(make_norm(V_FINAL, "Y", early_apply=not after_pool[len(phases)]))
        for t in range(NPT // 128):
            r0 = half * NPT + t * 128
            load_rows_fm(xp[seq, r0:r0 + 128, :], 128,
                         lambda hf, t=t: X[:, hf * 4:(hf + 1) * 4, t * 128:(t + 1) * 128],
                         xkeys("X", range(KC), [t // 4]))
            if t % 4 == 3:
                norms[0](t // 4)
        load_rows_fm(xs[sseq, :, :], NST, lambda hf: X[:, hf * 4:(hf + 1) * 4, NPT:NPT + NST], xkeys("X", range(KC), [2]))
        norms[0](2)
        for i in range(2):
            load_rows_fm(sta[i, sseq, :, :], 2, lambda hf, i=i: HA[i][:, hf * 4:(hf + 1) * 4, 1, :],
                         [("HA", i, 1, c) for c in range(KC)])
            load_rows_fm(stb[i, sseq, :, :], 30, lambda hf, i=i: HB[i][:, hf * 4:(hf + 1) * 4, 1, :],
                         [("HB", i, 1, c) for c in range(KC)])
            load_rows_fm(stp[i, sseq, :, :], 15, lambda hf, i=i: HP[i][:, hf * 4:(hf + 1) * 4, 1, :],
                         [("HP", i, 1, c) for c in range(KC)])
        for k, (_, _, body) in enumerate(phases):
            body(norms[k + 1], norms[k])
            norms[k].late()
        for t in range(NPT // 128):
            r0 = half * NPT + t * 128
            store_rows_tm(lambda c, t=t: B[:, c, t * 128:(t + 1) * 128], [("B", c) for c in range(KC)], 128,
                          yp[seq, r0:r0 + 128, :], None)
        store_rows_tm(lambda c: B[:, c, NPT:NPT + NST], [("B", c) for c in range(KC)], NST, ys[sseq, :, :], None)
        for i in range(2):
            store_rows_tm(lambda c, i=i: HA[i][:, c, 1, :], [("HA", i, 1, c) for c in range(KC)], 2, oas[i, sseq, :, :], None)
            store_rows_tm(lambda c, i=i: HB[i][:, c, 1, :], [("HB", i, 1, c) for c in range(KC)], 30, obs[i, sseq, :, :], None)
            store_rows_tm(lambda c, i=i: HP[i][:, c, 1, :], [("HP", i, 1, c) for c in range(KC)], 15, ops_[i, sseq, :, :], None)
            if not first_half:
                store_rows_tm(lambda c, i=i: HA[i][:, c, 0, :], [("HA", i, 0, c) for c in range(KC)], 2, oap[i, seq, :, :], None)
                store_rows_tm(lambda c, i=i: HB[i][:, c, 0, :], [("HB", i, 0, c) for c in range(KC)], 30, obp[i, seq, :, :], None)
                store_rows_tm(lambda c, i=i: HP[i][:, c, 0, :], [("HP", i, 0, c) for c in range(KC)], 15, opp[i, seq, :, :], None)

    final_waits = [tl.semval(tl.n) for tl in OSTL if tl.n > 0]

    with nc.Block() as block:
        @block.tensor
        def _(e):
            pe.replay(e)

        @block.scalar
        def _(e):
            act.replay(e)

        @block.vector
        def _(e):
            dve.replay(e)

        @block.gpsimd
        def _(e):
            pool.replay(e)

        @block.sync
        def _(e):
            sp.replay(e)
            for s, v in final_waits:
                e.wait_ge(s, v)

    stats = {k: len(v.ops) for k, v in (("pe", pe), ("act", act), ("dve", dve), ("pool", pool), ("sp", sp))}
    stats["sig"] = {k: v.tl.n for k, v in (("pe", pe), ("act", act), ("dve", dve))}
    return nc, stats


def _pack_vecs(norm_g, final_norm_g, conv_a_w, conv_b_w, ln_b_g, ln_b_b, pool_scale):
    rows = []
    rows += [norm_g[l, j] for l in range(DEPTH) for j in range(3)]
    rows += [final_norm_g]
    rows += [conv_a_w[i, k] for i in range(2) for k in range(3)]
    rows += [conv_b_w[i, k] for i in range(2) for k in range(31)]
    rows += [ln_b_g[i] for i in range(2)]
    rows += [ln_b_b[i] for i in range(2)]
    rows += [pool_scale[i] for i in range(2)]
    a = np.stack([np.asarray(r, dtype=np.float32) for r in rows])
    assert a.shape == (NVEC, D)
    return np.ascontiguousarray(a.reshape(NVEC, KC, 128).transpose(2, 0, 1).reshape(128, NVEC * KC))


def _consts():
    c = np.zeros((128, 322), dtype=np.float32)
    c[:, 320] = RMS_EPS
    c[:, 321] = LN_EPS
    c[:, 0:128] = np.eye(128, dtype=np.float32)
    c[:, 128:256] = 1.0 / D
    for g, win in enumerate((2, 4, 8, 16)):
        t = np.arange(16)
        c[:, 256 + g * 16:256 + (g + 1) * 16] = (1.0 / np.minimum(win, t + 1)).astype(np.float32)[None, :]
    return c


_CACHE = {}


def kernel(x_prompt, x_sample, state_conv_a, state_conv_b, state_pool, norm_g, final_norm_g,
           w_ffn_gate, w_ffn_up, w_ffn_down, w_in_conv, conv_a_w, conv_b_w, ln_b_g, ln_b_b,
           w_out_conv, w_pool, pool_scale):
    npass = int(os.environ.get("MK_NPASS", "4"))
    f = lambda a: np.ascontiguousarray(np.asarray(a, dtype=np.float32))
    x_prompt, x_sample = f(x_prompt), f(x_sample)
    state_conv_a, state_conv_b, state_pool = f(state_conv_a), f(state_conv_b), f(state_pool)
    vecs = _pack_vecs(f(norm_g), f(final_norm_g), f(conv_a_w), f(conv_b_w), f(ln_b_g), f(ln_b_b), f(pool_scale))
    cst = _consts()
    shared = {"vecs": vecs, "cst": cst, "wg": f(w_ffn_gate), "wu": f(w_ffn_up), "wd": f(w_ffn_down),
              "win": f(w_in_conv), "wout": f(w_out_conv), "wpool": f(w_pool)}
    if "nc" not in _CACHE or _CACHE.get("npass") != npass:
        _CACHE["nc"], _CACHE["stats"] = build_program(npass)
        _CACHE["npass"] = npass
    nc = _CACHE["nc"]
    in_maps = []
    for k in range(NCORES):
        m = dict(shared)
        m["xp"] = np.ascontiguousarray(x_prompt[2 * k:2 * k + 2])
        m["xs"] = np.ascontiguousarray(x_sample[4 * k:4 * k + 4])
        m["sta"] = np.ascontiguousarray(state_conv_a[:, 4 * k:4 * k + 4])
        m["stb"] = np.ascontiguousarray(state_conv_b[:, 4 * k:4 * k + 4])
        m["stp"] = np.ascontiguousarray(state_pool[:, 4 * k:4 * k + 4])
        in_maps.append(m)
    res = run_bass_kernel_spmd(nc, in_maps, core_ids=list(range(NCORES)))
    R = res.results
    cat = lambda key, ax: np.ascontiguousarray(np.concatenate([np.asarray(r[key], dtype=np.float32) for r in R], axis=ax))
    return (cat("yp", 0), cat("ys", 0), cat("oap", 1), cat("oas", 1), cat("obp", 1), cat("obs", 1),
            cat("opp", 1), cat("ops", 1))
```

```python
import os
import numpy as np
import concourse.bass as bass
import concourse.mybir as mybir
from concourse.bass_utils import run_bass_kernel_spmd

F32 = mybir.dt.float32
BF16 = mybir.dt.bfloat16
ALU = mybir.AluOpType
AF = mybir.ActivationFunctionType

NCORES = 8
D = 1024
KC = 8
FF = 4096
DEPTH = 4
SEQ = 2048
NPT = 1024
NST = 16
N = NPT + NST
SUB = [(0, 512), (512, 512), (1024, 16)]
HIST = 30
SEGS = [(0, NPT, HIST), (NPT, NST, HIST + NPT + HIST)]
EXT = HIST + NPT + HIST + NST
RMS_EPS = 1e-6
LN_EPS = 1e-5
NSLOT = 5
SLOTW = 4096
EPOCH = 6000
KD = 8

V_NORM = 0
V_FINAL = 12
V_CAW = 13
V_CBW = 19
V_LNG = 81
V_LNB = 83
V_PSC = 85
NVEC = 87


class TL:
    def __init__(self, nc, name, step):
        self.nc, self.name, self.step = nc, name, step
        self.n = 0
        self.sems = []

    def semval(self, cnt):
        e = (cnt - 1) // EPOCH
        while len(self.sems) <= e:
            self.sems.append(self.nc.alloc_semaphore(name=f"{self.name}_{len(self.sems)}"))
        return self.sems[e], ((cnt - 1) % EPOCH + 1) * self.step


class Eng:
    def __init__(self, nc, name, is_pe=False, attach=True):
        self.name = name
        self.tl = TL(nc, "t" + name, 1)
        self.ops = []
        self.seen = {}
        self.is_pe = is_pe
        self.attach = attach

    def force_signal(self):
        last = self.ops[-1]
        if last[2] is None:
            self.tl.n += 1
            last[2] = (self.tl.semval(self.tl.n)[0], 1)

    def emit(self, fn, deps, signal, dma_tl=None, dma_n=1):
        waits = []
        for tl, cnt in deps.items():
            if tl is self.tl and self.is_pe:
                continue
            if self.seen.get(tl, 0) >= cnt:
                continue
            self.seen[tl] = cnt
            waits.append(tl.semval(cnt))
        inc = None
        if dma_tl is not None:
            dma_tl.n += dma_n
            assert dma_tl.n < EPOCH
            inc = (dma_tl.semval(dma_tl.n)[0], 16)
            tok = (dma_tl, dma_tl.n)
            self.ops.append([waits, fn, inc])
        else:
            if signal:
                self.tl.n += 1
                inc = (self.tl.semval(self.tl.n)[0], 1)
                tok = (self.tl, self.tl.n)
            else:
                tok = (self.tl, self.tl.n + 1)
            self.ops.append([waits, fn, inc])
        return tok

    def replay(self, e):
        for waits, fn, inc in self.ops:
            if self.attach and waits:
                for s, v in waits[:-1]:
                    e.wait_ge(s, v)
                ins = fn(e)
                ins._wait_ge(*waits[-1])
            else:
                for s, v in waits:
                    e.wait_ge(s, v)
                ins = fn(e)
            if inc is not None:
                if isinstance(ins, (list, tuple)):
                    for i_ in ins:
                        i_.then_inc(inc[0], inc[1])
                else:
                    ins.then_inc(inc[0], inc[1])


class Prog:
    def __init__(self, nc):
        self.nc = nc
        self.pe = Eng(nc, "pe", is_pe=True, attach=False)
        self.act = Eng(nc, "act")
        self.dve = Eng(nc, "dve")
        self.pool = Eng(nc, "pool", attach=False)
        self.sp = Eng(nc, "sp", attach=False)
        self.engs = {e.tl: e for e in (self.pe, self.act, self.dve, self.pool, self.sp)}
        self.w = {}
        self.r = {}

    def _deps(self, reads, writes, me=None):
        d = {}

        def add(tl, c):
            if me is not None and me.is_pe and tl is me.tl:
                return
            eng = self.engs.get(tl)
            if eng is not None and c > tl.n:
                eng.force_signal()
            if d.get(tl, 0) < c:
                d[tl] = c

        for k in reads:
            if k in self.w:
                add(*self.w[k])
        for k in writes:
            if k in self.w:
                add(*self.w[k])
            for tl, c in self.r.get(k, {}).items():
                add(tl, c)
        return d

    def _commit(self, tok, reads, writes):
        tl, c = tok
        for k in reads:
            rr = self.r.setdefault(k, {})
            if rr.get(tl, 0) < c:
                rr[tl] = c
        for k in writes:
            self.w[k] = tok
            self.r[k] = {}

    def op(self, eng, fn, reads=(), writes=(), signal=False, dma_tl=None, dma_n=1):
        d = self._deps(reads, writes, eng)
        tok = eng.emit(fn, d, signal, dma_tl, dma_n)
        self._commit(tok, reads, writes)
        return tok


def build_program(npass=4):
    nc = bass.Bass("TRN2", target_bir_lowering=False)
    P = Prog(nc)
    pe, act, dve, pool, sp = P.pe, P.act, P.dve, P.pool, P.sp

    def dram(name, shape, kind="ExternalInput"):
        return nc.dram_tensor(name, list(shape), F32, kind=kind).ap()

    xp = dram("xp", [2, SEQ, D])
    xs = dram("xs", [4, NST, D])
    sta = dram("sta", [2, 4, 2, D])
    stb = dram("stb", [2, 4, 30, D])
    stp = dram("stp", [2, 4, 15, D])
    vecs = dram("vecs", [128, NVEC * 8])
    cst = dram("cst", [128, 128 + 128 + 64 + 2])
    wg = dram("wg", [DEPTH, 2, D, FF])
    wu = dram("wu", [DEPTH, 2, D, FF])
    wd = dram("wd", [DEPTH, 2, FF, D])
    win = dram("win", [2, D, 5 * D])
    wout = dram("wout", [2, 2 * D, D])
    wpool = dram("wpool", [2, 4, 256, 256])
    yp = dram("yp", [2, SEQ, D], "ExternalOutput")
    ys = dram("ys", [4, NST, D], "ExternalOutput")
    oap = dram("oap", [2, 2, 2, D], "ExternalOutput")
    oas = dram("oas", [2, 4, 2, D], "ExternalOutput")
    obp = dram("obp", [2, 2, 30, D], "ExternalOutput")
    obs = dram("obs", [2, 4, 30, D], "ExternalOutput")
    opp = dram("opp", [2, 2, 15, D], "ExternalOutput")
    ops_ = dram("ops", [2, 4, 15, D], "ExternalOutput")

    def sb(name, shape, dt):
        return nc.alloc_sbuf_tensor(name, list(shape), dt)

    X = sb("X", [128, KC, N], F32)
    U = sb("U", [128, KC, N], BF16)
    H = sb("H", [128, KC, N], BF16)
    B = sb("B", [128, KC, EXT], F32)
    WS = [sb(f"W{i}", [128, SLOTW], BF16) for i in range(NSLOT)]
    WTL = [TL(nc, f"wl{i}", 16) for i in range(NSLOT)]
    E32 = [sb(f"E32_{i}", [128, EXT], F32) for i in range(2)]
    E16 = [sb(f"E16_{i}", [128, EXT], BF16) for i in range(2)]
    S1 = [sb(f"S1_{i}", [128, N], F32) for i in range(2)]
    S2 = [sb(f"S2_{i}", [128, N], BF16) for i in range(2)]
    DG = sb("DG", [128, 31, 128], BF16)
    V = sb("V", [128, NVEC * 8], F32)
    CST = sb("CST", [128, 322], F32)
    IDB = sb("IDB", [128, 128], BF16)
    ONB = sb("ONB", [128, 128], BF16)
    IOB = [sb(f"IOB{i}", [128, D], F32) for i in range(2)]
    IOTL = [TL(nc, f"iol{i}", 16) for i in range(2)]
    OSTL = [TL(nc, f"ios{i}", 16) for i in range(2)]
    CTL = TL(nc, "cl", 16)
    R1 = sb("R1", [128, 512], F32)
    R2 = sb("R2", [128, 512], F32)
    RS = sb("RS", [128, 512], F32)
    RS3 = [RS, sb("RSb", [128, 512], F32), sb("RSc", [128, 16], F32)]
    MU = sb("MU", [128, 512], F32)
    MU3 = [MU, sb("MUb", [128, 512], F32), sb("MUc", [128, 16], F32)]
    SG = [sb(f"SG{i}", [128, 512], F32) for i in range(2)]
    HA = [sb(f"HA{i}", [128, KC, 2, 2], F32) for i in range(2)]
    HB = [sb(f"HB{i}", [128, KC, 2, 30], F32) for i in range(2)]
    HP = [sb(f"HP{i}", [128, KC, 2, 15], F32) for i in range(2)]
    PS = [nc.alloc_psum_tensor(f"ps{i}", [128, 512], F32) for i in range(8)]

    IDENT = CST[:, 0:128]
    ONES32 = CST[:, 128:256]
    INVC = CST[:, 256:320]
    EPS_RMS = CST[:, 320:321]
    EPS_LN = CST[:, 321:322]

    st = {"bank": 0, "slot": 0, "io": 0, "sg": 0}

    def nbank():
        b = st["bank"]
        st["bank"] = (b + 1) % 8
        return b

    def vcol(vec, c):
        col = vec * 8 + c
        return V[:, col:col + 1]

    def xkeys(name, cs, ss):
        return [(name, c, s) for c in cs for s in ss]

    ALLS = range(len(SUB))

    def mm(out, lhsT, rhs, start, stop, reads, writes, signal=False):
        return P.op(pe, lambda e: e.matmul(out, lhsT, rhs, start=start, stop=stop), reads, writes, signal)

    def tr(out, in_, ident, reads, writes, signal=False):
        return P.op(pe, lambda e: e.transpose(out, in_, ident), reads, writes, signal)

    def load_slab(dst_fn, src):
        k = st["slot"] % NSLOT
        st["slot"] += 1
        if isinstance(src, list):
            pairs = [(dst_fn(WS[k], j), sj) for j, sj in enumerate(src)]
        else:
            pairs = [(dst_fn(WS[k]), src)]
        P.op(pool, lambda e: [e.dma_start(out=d_, in_=s_) for d_, s_ in pairs], (), [("W", k)], dma_tl=WTL[k],
             dma_n=len(pairs))
        return k

    def io_buf():
        b = st["io"] % 2
        st["io"] += 1
        return b

    P.op(sp, lambda e: e.dma_start(out=V[:, :], in_=vecs[:, :]), (), ["V"], dma_tl=CTL)
    P.op(sp, lambda e: e.dma_start(out=CST[:, :], in_=cst[:, :]), (), ["CST"], dma_tl=TL(nc, "cl2", 16))
    P.op(act, lambda e: e.copy(out=IDB[:, :], in_=IDENT), ["CST"], ["IDB"])
    P.op(dve, lambda e: e.memset(ONB[:, :], 1.0), (), ["ONB"])
    for i in range(2):
        P.op(dve, lambda e, i=i: e.memset(E32[i][:, :], 0.0), (), [("E32", i)])
    for c in range(KC):
        P.op(dve, lambda e, c=c: e.memset(B[:, c, :], 0.0), (), [("B", c)])

    def make_norm(vec, dst_kind, early_apply=True, defer_late=False):
        banks = {}

        def sq(s):
            t0, n = SUB[s]
            P.op(act, lambda e: e.activation(out=U[:, :, t0:t0 + n], in_=X[:, :, t0:t0 + n], func=AF.Square),
                 xkeys("X", range(KC), [s]), xkeys("U", range(KC), [s]), signal=True)

        def stats(s):
            t0, n = SUB[s]
            b = nbank()
            banks[s] = b
            for c in range(KC):
                mm(PS[b][:, 0:n], ONB[:, :], U[:, c, t0:t0 + n], c == 0, c == KC - 1, [("U", c, s), "ONB"], [("ps", b)],
                   signal=(c == KC - 1))

        def chain(s):
            t0, n = SUB[s]
            b = banks[s]
            P.op(act, lambda e: e.activation(out=R1[:, 0:n], in_=PS[b][:, 0:n], func=AF.Ln, bias=EPS_RMS, scale=1.0 / D),
                 [("ps", b), "CST"], ["R1"], signal=True)
            P.op(act, lambda e: e.activation(out=RS3[s][:, 0:n], in_=R1[:, 0:n], func=AF.Exp, scale=-0.5),
                 ["R1"], [("RS", s)], signal=True)

        def apply(s):
            t0, n = SUB[s]
            for c in range(KC):
                if dst_kind == "U":
                    out, wk = U[:, c, t0:t0 + n], ("U", c, s)
                elif dst_kind == "Y":
                    out, wk = B[:, c, t0:t0 + n], ("B", c)
                else:
                    seg = 0 if t0 < NPT else 1
                    e0 = SEGS[seg][2] + (t0 - SEGS[seg][0])
                    out, wk = B[:, c, e0:e0 + n], ("B", c)
                rk = [("X", c, s), ("RS", s), "V"] + ([] if dst_kind == "U" else [("U", c, s)])
                P.op(dve, lambda e, c=c, out=out: e.scalar_tensor_tensor(
                    out=out, in0=X[:, c, t0:t0 + n], scalar=vcol(vec, c), in1=RS3[s][:, 0:n],
                    op0=ALU.mult, op1=ALU.mult), rk, [wk], signal=True)

        last = len(SUB) - 1
        flags = {"mid": False}

        def mid_sub(s):
            if s == 1 and not flags["mid"]:
                flags["mid"] = True
                stats(0)
                chain(0)
                if early_apply:
                    apply(0)

        def tail_part():
            if not early_apply:
                apply(0)
            for t in range(1, last + 1):
                stats(t)
                chain(t)
            for t in range(1, last + 1):
                apply(t)

        def late():
            if flags.get("late"):
                flags["late"] = False
                tail_part()

        def after_sub(s):
            sq(s)
            if s == 1:
                mid_sub(1)
            if s == last:
                if defer_late and early_apply:
                    flags["late"] = True
                else:
                    tail_part()

        after_sub.late = late
        after_sub.mid = mid_sub
        return after_sub

    def ffn(l, f, tail_cb, own=None):
        NG = FF // 512

        def gateup(j, hook=None):
            kg = load_slab(lambda w: w[:, 0:4096].rearrange("p (k n) -> p k n", k=KC),
                           wg[l, f, :, j * 512:(j + 1) * 512].rearrange("(k p) n -> p k n", p=128))
            ku = load_slab(lambda w: w[:, 0:4096].rearrange("p (k n) -> p k n", k=KC),
                           wu[l, f, :, j * 512:(j + 1) * 512].rearrange("(k p) n -> p k n", p=128))
            hb = (j % 2) * 4
            for s, (t0, n) in enumerate(SUB):
                for m in range(4):
                    bg = nbank()
                    bu = nbank()
                    for k in range(KC):
                        mm(PS[bg][:, 0:n], WS[kg][:, k * 512 + m * 128:k * 512 + (m + 1) * 128], U[:, k, t0:t0 + n],
                           k == 0, k == KC - 1, [("W", kg), ("U", k, s)], [("ps", bg)], signal=(k == KC - 1))
                    for k in range(KC):
                        mm(PS[bu][:, 0:n], WS[ku][:, k * 512 + m * 128:k * 512 + (m + 1) * 128], U[:, k, t0:t0 + n],
                           k == 0, k == KC - 1, [("W", ku), ("U", k, s)], [("ps", bu)], signal=(k == KC - 1))
                    g = st["sg"] % 2
                    st["sg"] += 1
                    P.op(act, lambda e, bg=bg, g=g, n=n: e.activation(out=SG[g][:, 0:n], in_=PS[bg][:, 0:n], func=AF.Silu),
                         [("ps", bg)], [("SG", g)], signal=True)
                    P.op(dve, lambda e, bu=bu, g=g, n=n, t0=t0, c=hb + m: e.tensor_tensor(
                        out=H[:, c, t0:t0 + n], in0=PS[bu][:, 0:n], in1=SG[g][:, 0:n], op=ALU.mult),
                        [("ps", bu), ("SG", g)], [("H", hb + m, s)], signal=True)
                    if hook is not None and s == 0 and m == 0:
                        hook()

        def load_wd(j):
            return load_slab(lambda w: w[:, 0:4096].rearrange("p (k n) -> p k n", k=4),
                             wd[l, f, j * 512:(j + 1) * 512, :].rearrange("(k p) n -> p k n", p=128))

        def down_sub(j, kd, s):
            hb = (j % 2) * 4
            t0, n = SUB[s]
            for mo in range(KC):
                b = nbank()
                for kk in range(4):
                    mm(PS[b][:, 0:n], WS[kd][:, kk * 1024 + mo * 128:kk * 1024 + (mo + 1) * 128],
                       H[:, hb + kk, t0:t0 + n], kk == 0, kk == 3, [("W", kd), ("H", hb + kk, s)], [("ps", b)],
                       signal=(kk == 3))
                P.op(dve, lambda e, b=b, mo=mo: e.scalar_tensor_tensor(
                    out=X[:, mo, t0:t0 + n], in0=PS[b][:, 0:n], scalar=0.5, in1=X[:, mo, t0:t0 + n],
                    op0=ALU.mult, op1=ALU.add), [("ps", b), ("X", mo, s)], [("X", mo, s)], signal=True)

        def down(j):
            kd = load_wd(j)
            for s in range(len(SUB)):
                down_sub(j, kd, s)

        gateup(0, hook=(own.late if own is not None else None))
        for j in range(1, NG):
            gateup(j)
            if j < NG - 1:
                down(j - 1)
        kda = load_wd(NG - 2)
        kdb = load_wd(NG - 1)
        for s in range(len(SUB)):
            down_sub(NG - 2, kda, s)
            if tail_cb is not None and s >= 1:
                tail_cb.mid(s)
            down_sub(NG - 1, kdb, s)
            if tail_cb is not None:
                tail_cb(s)

    def seg_of_sub(s):
        t0, n = SUB[s]
        seg = 0 if t0 < NPT else 1
        return seg, SEGS[seg][2] + (t0 - SEGS[seg][0])

    def conv_mixer(i, first_half, tail_cb, own=None):
        l = 2 * i

        def inproj(c, part, hook=None):
            q = c % 2
            if part == "A":
                wv5 = win[i].rearrange("(k p) (j c n) -> p k j c n", p=128, j=5, c=8)
                kw = load_slab(lambda w, j: w[:, 0:3072].rearrange("p (k j n) -> p k j n", k=KC, j=3)[:, :, j, :],
                               [wv5[:, :, j, c, :] for j in range(3)])
                nj = 3
            else:
                wv5 = win[i].rearrange("(k p) (j c n) -> p k j c n", p=128, j=5, c=8)
                kw = load_slab(lambda w, j: w[:, 0:2048].rearrange("p (k j n) -> p k j n", k=KC, j=2)[:, :, j, :],
                               [wv5[:, :, 3 + j, c, :] for j in range(2)])
                nj = 2
            for s, (t0, n) in enumerate(SUB):
                seg, e0 = seg_of_sub(s)
                banks = [nbank() for _ in range(nj)]
                for j in range(nj):
                    for k in range(KC):
                        mm(PS[banks[j]][:, 0:n], WS[kw][:, (k * nj + j) * 128:(k * nj + j + 1) * 128], U[:, k, t0:t0 + n],
                           k == 0, k == KC - 1, [("W", kw), ("U", k, s)], [("ps", banks[j])], signal=(k == KC - 1))
                if part == "A":
                    bh, bb, bc = banks
                    P.op(act, lambda e, bh=bh, q=q, t0=t0, n=n: e.copy(out=S1[q][:, t0:t0 + n], in_=PS[bh][:, 0:n]),
                         [("ps", bh)], [("S1", q)], signal=True)
                    P.op(dve, lambda e, bc=bc, q=q, t0=t0, n=n, e0=e0: e.tensor_tensor(
                        out=E32[q][:, e0:e0 + n], in0=PS[bc][:, 0:n], in1=S1[q][:, t0:t0 + n], op=ALU.mult),
                        [("ps", bc), ("S1", q)], [("E32", q)], signal=True)
                    P.op(act, lambda e, bb=bb, q=q, t0=t0, n=n: e.copy(out=S2[q][:, t0:t0 + n], in_=PS[bb][:, 0:n]),
                         [("ps", bb)], [("S2", q)], signal=True)
                    if hook is not None and s == 0:
                        hook()
                else:
                    bv, bgt = banks
                    P.op(act, lambda e, bgt=bgt, q=q, t0=t0, n=n: e.activation(out=S1[q][:, t0:t0 + n], in_=PS[bgt][:, 0:n],
                                                                               func=AF.Sigmoid),
                         [("ps", bgt)], [("S1", q)], signal=True)
                    P.op(dve, lambda e, bv=bv, q=q, t0=t0, n=n, e0=e0: e.tensor_tensor(
                        out=E32[q][:, e0:e0 + n], in0=PS[bv][:, 0:n], in1=S1[q][:, t0:t0 + n], op=ALU.mult),
                        [("ps", bv), ("S1", q)], [("E32", q)], signal=True)
            hw = 2 if part == "A" else 30
            HH = HA[i] if part == "A" else HB[i]
            hname = "HA" if part == "A" else "HB"
            for seg, (toff, ln, eoff) in enumerate(SEGS):
                if seg == 0 and first_half:
                    P.op(dve, lambda e, q=q, eoff=eoff, hw=hw: e.memset(E32[q][:, eoff - hw:eoff], 0.0),
                         (), [("E32", q)])
                else:
                    P.op(act, lambda e, q=q, eoff=eoff, hw=hw, seg=seg, HH=HH, c=c: e.copy(
                        out=E32[q][:, eoff - hw:eoff], in_=HH[:, c, seg, :]), [(hname, i, seg, c)], [("E32", q)])
            P.op(act, lambda e, q=q: e.copy(out=E16[q][:, :], in_=E32[q][:, :]), [("E32", q)], [("E16", q)], signal=True)
            for seg, (toff, ln, eoff) in enumerate(SEGS):
                P.op(act, lambda e, q=q, eoff=eoff, ln=ln, hw=hw, seg=seg, HH=HH, c=c: e.copy(
                    out=HH[:, c, seg, :], in_=E32[q][:, eoff + ln - hw:eoff + ln]), [("E32", q)], [(hname, i, seg, c)])
            if part == "B":
                dve_taps(c)

        def build_dg(c, part):
            ntap = 3 if part == "A" else 31
            wv = (V_CAW + i * 3) if part == "A" else (V_CBW + i * 31)
            for k in range(0 if part == "A" else KD, ntap):
                P.op(act, lambda e, k=k, c=c, wv=wv: e.activation(out=DG[:, k, :], in_=IDB[:, :], func=AF.Copy,
                                                                   scale=vcol(wv + k, c)),
                     ["IDB", "V"], [("DG", k)], signal=(k == ntap - 1))

        def conv(c, part):
            q = c % 2
            ntap = 3 if part == "A" else 31
            k0 = 0 if part == "A" else KD
            for s, (t0, n) in enumerate(SUB):
                seg, e0 = seg_of_sub(s)
                b = nbank()
                for k in range(k0, ntap):
                    off = e0 - (ntap - 1) + k
                    mm(PS[b][:, 0:n], DG[:, k, :], E16[q][:, off:off + n], k == k0, k == ntap - 1,
                       [("DG", k), ("E16", q)], [("ps", b)], signal=(k == ntap - 1))
                if part == "A":
                    P.op(dve, lambda e, b=b, q=q, c=c, t0=t0, n=n: e.tensor_tensor(
                        out=H[:, c, t0:t0 + n], in0=PS[b][:, 0:n], in1=S2[q][:, t0:t0 + n], op=ALU.mult),
                        [("ps", b), ("S2", q)], [("H", c, s)], signal=True)
                else:
                    P.op(dve, lambda e, b=b, c=c, e0=e0, n=n: e.tensor_tensor(
                        out=B[:, c, e0:e0 + n], in0=PS[b][:, 0:n], in1=B[:, c, e0:e0 + n], op=ALU.add),
                        [("ps", b), ("B", c)], [("B", c)], signal=True)

        def dve_taps(c):
            q = c % 2
            wv = V_CBW + i * 31
            w_ = EXT - HIST
            for k in range(KD):
                if k == 0:
                    P.op(dve, lambda e, k=k: e.tensor_scalar(out=B[:, c, HIST:EXT], in0=E32[q][:, k:k + w_],
                                                            scalar1=vcol(wv + k, c), scalar2=None, op0=ALU.mult),
                         [("E32", q), "V"], [("B", c)], signal=True)
                else:
                    P.op(dve, lambda e, k=k: e.scalar_tensor_tensor(out=B[:, c, HIST:EXT], in0=E32[q][:, k:k + w_],
                                                                     scalar=vcol(wv + k, c), in1=B[:, c, HIST:EXT],
                                                                     op0=ALU.mult, op1=ALU.add),
                         [("E32", q), "V", ("B", c)], [("B", c)], signal=True)

        for part in ("A", "B"):
            inproj(0, part, hook=(own.late if (own is not None and part == "A") else None))
            build_dg(0, part)
            for c in range(1, KC):
                inproj(c, part)
                conv(c - 1, part)
                build_dg(c, part)
            conv(KC - 1, part)

        def wout_part(s, part, kos, mid=None):
            t0, n = SUB[s]
            src, sname = (H, "H") if part == "A" else (U, "U")
            for mo in range(KC):
                if mo == KC // 2 and mid is not None:
                    mid(s)
                b = nbank()
                for kc in range(8):
                    ko = kos[kc // 4]
                    kk = kc % 4
                    mm(PS[b][:, 0:n], WS[ko][:, kk * 1024 + mo * 128:kk * 1024 + (mo + 1) * 128], src[:, kc, t0:t0 + n],
                       kc == 0, kc == 7, [("W", ko), (sname, kc, s)], [("ps", b)], signal=(kc == 7))
                P.op(dve, lambda e, b=b, mo=mo: e.tensor_tensor(
                    out=X[:, mo, t0:t0 + n], in0=PS[b][:, 0:n], in1=X[:, mo, t0:t0 + n], op=ALU.add),
                    [("ps", b), ("X", mo, s)], [("X", mo, s)], signal=True)

        def load_wout(qq):
            return load_slab(lambda w: w[:, 0:4096].rearrange("p (k n) -> p k n", k=4),
                             wout[i, qq * 512:(qq + 1) * 512, :].rearrange("(k p) n -> p k n", p=128))

        kosA = [load_wout(0), load_wout(1)]
        for s, (t0, n) in enumerate(SUB):
            e0 = seg_of_sub(s)[1]
            P.op(act, lambda e, t0=t0, n=n, e0=e0: e.activation(out=U[:, :, t0:t0 + n], in_=B[:, :, e0:e0 + n], func=AF.Square),
                 [("B", c) for c in range(KC)], xkeys("U", range(KC), [s]), signal=True)
        for s, (t0, n) in enumerate(SUB):
            e0 = seg_of_sub(s)[1]
            bs = nbank()
            bq = nbank()
            for c in range(KC):
                mm(PS[bs][:, 0:n], ONES32, B[:, c, e0:e0 + n], c == 0, c == KC - 1, [("B", c), "CST"], [("ps", bs)],
                   signal=(c == KC - 1))
            for c in range(KC):
                mm(PS[bq][:, 0:n], ONB[:, :], U[:, c, t0:t0 + n], c == 0, c == KC - 1, [("U", c, s), "ONB"], [("ps", bq)],
                   signal=(c == KC - 1))
            mu, rs = MU3[s], RS3[s]
            P.op(act, lambda e, bs=bs, n=n, mu=mu: e.copy(out=mu[:, 0:n], in_=PS[bs][:, 0:n]), [("ps", bs)], [("MU", s)], signal=True)
            P.op(dve, lambda e, n=n, mu=mu: e.tensor_tensor(out=R1[:, 0:n], in0=mu[:, 0:n], in1=mu[:, 0:n], op=ALU.mult),
                 [("MU", s)], ["R1"], signal=True)
            P.op(dve, lambda e, bq=bq, n=n: e.scalar_tensor_tensor(out=R2[:, 0:n], in0=PS[bq][:, 0:n], scalar=1.0 / D,
                                                                      in1=R1[:, 0:n], op0=ALU.mult, op1=ALU.subtract),
                 [("ps", bq), "R1"], ["R2"], signal=True)
            P.op(dve, lambda e, n=n: e.tensor_scalar(out=R2[:, 0:n], in0=R2[:, 0:n], scalar1=0.0, scalar2=None, op0=ALU.max),
                 ["R2"], ["R2"], signal=True)
            P.op(act, lambda e, n=n: e.activation(out=R1[:, 0:n], in_=R2[:, 0:n], func=AF.Ln, bias=EPS_LN, scale=1.0),
                 ["R2", "CST"], ["R1"], signal=True)
            P.op(act, lambda e, n=n, rs=rs: e.activation(out=rs[:, 0:n], in_=R1[:, 0:n], func=AF.Exp, scale=-0.5),
                 ["R1"], [("RS", s)], signal=True)
            for c in range(KC):
                g = st["sg"] % 2
                st["sg"] += 1
                P.op(dve, lambda e, g=g, c=c, e0=e0, n=n, mu=mu: e.tensor_tensor(out=SG[g][:, 0:n], in0=B[:, c, e0:e0 + n],
                                                                                 in1=mu[:, 0:n], op=ALU.subtract),
                     [("B", c), ("MU", s)], [("SG", g)], signal=True)
                P.op(dve, lambda e, g=g, n=n, rs=rs: e.tensor_tensor(out=SG[g][:, 0:n], in0=SG[g][:, 0:n], in1=rs[:, 0:n], op=ALU.mult),
                     [("SG", g), ("RS", s)], [("SG", g)], signal=True)
                P.op(act, lambda e, g=g, c=c, t0=t0, n=n: e.activation(
                    out=U[:, c, t0:t0 + n], in_=SG[g][:, 0:n], func=AF.Silu, bias=vcol(V_LNB + i, c), scale=vcol(V_LNG + i, c)),
                    [("SG", g), "V"], [("U", c, s)], signal=True)
            wout_part(s, "A", kosA)
        kosB = [load_wout(2), load_wout(3)]
        for s in range(len(SUB)):
            wout_part(s, "B", kosB, mid=(tail_cb.mid if (tail_cb is not None and s >= 1) else None))
            if tail_cb is not None:
                tail_cb(s)

    def pool_mixer(i, first_half, tail_cb):
        l = 2 * i + 1
        wp4 = wpool[i].rearrange("g (k p) o -> p g k o", p=128)
        kw = load_slab(lambda w, g: w[:, 0:2048].rearrange("p (g k o) -> p g k o", g=4, k=2)[:, g, :, :],
                       [wp4[:, g, :, :] for g in range(4)])
        for c in range(KC):
            for seg, (toff, ln, eoff) in enumerate(SEGS):
                if seg == 0 and first_half:
                    P.op(dve, lambda e, c=c, eoff=eoff: e.memset(B[:, c, eoff - 15:eoff], 0.0), (), [("B", c)])
                else:
                    P.op(act, lambda e, c=c, eoff=eoff, seg=seg: e.copy(out=B[:, c, eoff - 15:eoff], in_=HP[i][:, c, seg, :]),
                         [("HP", i, seg, c)], [("B", c)])
            for seg, (toff, ln, eoff) in enumerate(SEGS):
                P.op(act, lambda e, c=c, eoff=eoff, ln=ln, seg=seg: e.copy(out=HP[i][:, c, seg, :],
                                                                                  in_=B[:, c, eoff + ln - 15:eoff + ln]),
                     [("B", c)], [("HP", i, seg, c)])
            g = c // 2
            win_ = 2 << g
            src = B[:, c, :]
            srck = ("B", c)
            sh = 1
            q = 0
            lo = 15
            for step in range(g + 1):
                dst = E32[q]
                P.op(dve, lambda e, dst=dst, src=src, sh=sh, lo=lo: e.tensor_tensor(
                    out=dst[:, lo:EXT], in0=src[:, lo:EXT], in1=src[:, lo - sh:EXT - sh], op=ALU.add),
                    [srck], [("E32", q)], signal=True)
                src = dst
                srck = ("E32", q)
                sh *= 2
                q ^= 1
            for seg, (toff, ln, eoff) in enumerate(SEGS):
                P.op(dve, lambda e, src=src, c=c, toff=toff, ln=ln, eoff=eoff, win_=win_: e.scalar_tensor_tensor(
                    out=U[:, c, toff:toff + ln], in0=src[:, eoff:eoff + ln], scalar=1.0 / win_, in1=B[:, c, eoff:eoff + ln],
                    op0=ALU.mult, op1=ALU.subtract), [srck, ("B", c)], xkeys("U", [c], ALLS), signal=True)
                if seg == 0 and first_half:
                    P.op(dve, lambda e, src=src, eoff=eoff, g=g: e.tensor_tensor(
                        out=R1[:, 0:16], in0=src[:, eoff:eoff + 16], in1=INVC[:, g * 16:(g + 1) * 16], op=ALU.mult),
                        [srck, "CST"], ["R1"])
                    P.op(dve, lambda e, c=c, toff=toff, eoff=eoff: e.tensor_tensor(
                        out=U[:, c, toff:toff + 16], in0=R1[:, 0:16], in1=B[:, c, eoff:eoff + 16], op=ALU.subtract),
                        ["R1", ("B", c)], xkeys("U", [c], ALLS), signal=True)
        for s, (t0, n) in enumerate(SUB):
            for g in range(4):
                for mh in range(2):
                    mo = 2 * g + mh
                    b = nbank()
                    for kc in range(2):
                        base = (g * 2 + kc) * 256 + mh * 128
                        mm(PS[b][:, 0:n], WS[kw][:, base:base + 128], U[:, 2 * g + kc, t0:t0 + n], kc == 0, kc == 1,
                           [("W", kw), ("U", 2 * g + kc, s)], [("ps", b)], signal=(kc == 1))
                    P.op(dve, lambda e, b=b, mo=mo, t0=t0, n=n: e.scalar_tensor_tensor(
                        out=X[:, mo, t0:t0 + n], in0=PS[b][:, 0:n], scalar=vcol(V_PSC + i, mo), in1=X[:, mo, t0:t0 + n],
                        op0=ALU.mult, op1=ALU.add), [("ps", b), ("X", mo, s), "V"], [("X", mo, s)], signal=True)
            if tail_cb is not None:
                tail_cb(s)

    def load_rows_fm(src_rows, r, dst_fn, dst_keys):
        ib = io_buf()
        P.op(sp, lambda e: e.dma_start(out=IOB[ib][0:r, :], in_=src_rows), (), [("IOB", ib)], dma_tl=IOTL[ib])
        for half in range(2):
            b = nbank()
            for cc in range(4):
                c = half * 4 + cc
                tr(PS[b][:, cc * r:(cc + 1) * r], IOB[ib][0:r, c * 128:(c + 1) * 128], IDENT[0:r, 0:r],
                   [("IOB", ib), "CST"], [("ps", b)], signal=(cc == 3))
            eng = act if half == 0 else dve
            src = PS[b][:, 0:4 * r].rearrange("p (c t) -> p c t", c=4)
            if eng is act:
                P.op(act, lambda e, src=src, half=half: e.copy(out=dst_fn(half), in_=src), [("ps", b)], dst_keys, signal=True)
            else:
                P.op(dve, lambda e, src=src, half=half: e.tensor_copy(out=dst_fn(half), in_=src), [("ps", b)], dst_keys, signal=True)

    def store_rows_tm(src_fn, src_keys, r, dst_rows, out_tls):
        ib = io_buf()
        for half in range(2):
            b = nbank()
            for cc in range(4):
                c = half * 4 + cc
                tr(PS[b][0:r, cc * 128:(cc + 1) * 128], src_fn(c), IDENT, src_keys + ["CST"], [("ps", b)], signal=(cc == 3))
            if half == 0:
                P.op(act, lambda e, b=b: e.copy(out=IOB[ib][0:r, 0:512], in_=PS[b][0:r, 0:512]), [("ps", b)], [("IOB", ib)], signal=True)
            else:
                P.op(dve, lambda e, b=b: e.tensor_copy(out=IOB[ib][0:r, 512:1024], in_=PS[b][0:r, 0:512]), [("ps", b)], [("IOB", ib)], signal=True)
        tok = P.op(sp, lambda e: e.dma_start(out=dst_rows, in_=IOB[ib][0:r, :]), [("IOB", ib)], (), dma_tl=OSTL[ib])
        return tok

    for p in range(npass):
        seq, half = p // 2, p % 2
        first_half = (half == 0)
        sseq = p
        phases = []
        for l in range(DEPTH):
            phases.append((V_NORM + l * 3 + 0, "U", lambda cb, own, l=l: ffn(l, 0, cb, own)))
            if l % 2 == 0:
                phases.append((V_NORM + l * 3 + 1, "U", lambda cb, own, l=l: conv_mixer(l // 2, first_half, cb, own)))
            else:
                phases.append((V_NORM + l * 3 + 1, "B", lambda cb, own, l=l: pool_mixer(l // 2, first_half, cb)))
            phases.append((V_NORM + l * 3 + 2, "U", lambda cb, own, l=l: ffn(l, 1, cb, own)))
        after_pool = [False] + [kind == "B" for _, kind, _ in phases]
        norms = [make_norm(v, k, early_apply=not after_pool[j], defer_late=(k == "U")) for j, (v, k, _) in enumerate(phases)]
        norms.append(make_norm(V_FINAL, "Y", early_apply=not after_pool[len(phases)]))
        for t in range(NPT // 128):
            r0 = half * NPT + t * 128
            load_rows_fm(xp[seq, r0:r0 + 128, :], 128,
                         lambda hf, t=t: X[:, hf * 4:(hf + 1) * 4, t * 128:(t + 1) * 128],
                         xkeys("X", range(KC), [t // 4]))
            if t % 4 == 3:
                norms[0](t // 4)
        load_rows_fm(xs[sseq, :, :], NST, lambda hf: X[:, hf * 4:(hf + 1) * 4, NPT:NPT + NST], xkeys("X", range(KC), [2]))
        norms[0](2)
        for i in range(2):
            load_rows_fm(sta[i, sseq, :, :], 2, lambda hf, i=i: HA[i][:, hf * 4:(hf + 1) * 4, 1, :],
                         [("HA", i, 1, c) for c in range(KC)])
            load_rows_fm(stb[i, sseq, :, :], 30, lambda hf, i=i: HB[i][:, hf * 4:(hf + 1) * 4, 1, :],
                         [("HB", i, 1, c) for c in range(KC)])
            load_rows_fm(stp[i, sseq, :, :], 15, lambda hf, i=i: HP[i][:, hf * 4:(hf + 1) * 4, 1, :],
                         [("HP", i, 1, c) for c in range(KC)])
        for k, (_, _, body) in enumerate(phases):
            body(norms[k + 1], norms[k])
            norms[k].late()
        for t in range(NPT // 128):
            r0 = half * NPT + t * 128
            store_rows_tm(lambda c, t=t: B[:, c, t * 128:(t + 1) * 128], [("B", c) for c in range(KC)], 128,
                          yp[seq, r0:r0 + 128, :], None)
        store_rows_tm(lambda c: B[:, c, NPT:NPT + NST], [("B", c) for c in range(KC)], NST, ys[sseq, :, :], None)
        for i in range(2):
            store_rows_tm(lambda c, i=i: HA[i][:, c, 1, :], [("HA", i, 1, c) for c in range(KC)], 2, oas[i, sseq, :, :], None)
            store_rows_tm(lambda c, i=i: HB[i][:, c, 1, :], [("HB", i, 1, c) for c in range(KC)], 30, obs[i, sseq, :, :], None)
            store_rows_tm(lambda c, i=i: HP[i][:, c, 1, :], [("HP", i, 1, c) for c in range(KC)], 15, ops_[i, sseq, :, :], None)
            if not first_half:
                store_rows_tm(lambda c, i=i: HA[i][:, c, 0, :], [("HA", i, 0, c) for c in range(KC)], 2, oap[i, seq, :, :], None)
                store_rows_tm(lambda c, i=i: HB[i][:, c, 0, :], [("HB", i, 0, c) for c in range(KC)], 30, obp[i, seq, :, :], None)
                store_rows_tm(lambda c, i=i: HP[i][:, c, 0, :], [("HP", i, 0, c) for c in range(KC)], 15, opp[i, seq, :, :], None)

    final_waits = [tl.semval(tl.n) for tl in OSTL if tl.n > 0]

    with nc.Block() as block:
        @block.tensor
        def _(e):
            pe.replay(e)

        @block.scalar
        def _(e):
            act.replay(e)

        @block.vector
        def _(e):
            dve.replay(e)

        @block.gpsimd
        def _(e):
            pool.replay(e)

        @block.sync
        def _(e):
            sp.replay(e)
            for s, v in final_waits:
                e.wait_ge(s, v)

    stats = {k: len(v.ops) for k, v in (("pe", pe), ("act", act), ("dve", dve), ("pool", pool), ("sp", sp))}
    stats["sig"] = {k: v.tl.n for k, v in (("pe", pe), ("act", act), ("dve", dve))}
    return nc, stats


def _pack_vecs(norm_g, final_norm_g, conv_a_w, conv_b_w, ln_b_g, ln_b_b, pool_scale):
    rows = []
    rows += [norm_g[l, j] for l in range(DEPTH) for j in range(3)]
    rows += [final_norm_g]
    rows += [conv_a_w[i, k] for i in range(2) for k in range(3)]
    rows += [conv_b_w[i, k] for i in range(2) for k in range(31)]
    rows += [ln_b_g[i] for i in range(2)]
    rows += [ln_b_b[i] for i in range(2)]
    rows += [pool_scale[i] for i in range(2)]
    a = np.stack([np.asarray(r, dtype=np.float32) for r in rows])
    assert a.shape == (NVEC, D)
    return np.ascontiguousarray(a.reshape(NVEC, KC, 128).transpose(2, 0, 1).reshape(128, NVEC * KC))


def _consts():
    c = np.zeros((128, 322), dtype=np.float32)
    c[:, 320] = RMS_EPS
    c[:, 321] = LN_EPS
    c[:, 0:128] = np.eye(128, dtype=np.float32)
    c[:, 128:256] = 1.0 / D
    for g, win in enumerate((2, 4, 8, 16)):
        t = np.arange(16)
        c[:, 256 + g * 16:256 + (g + 1) * 16] = (1.0 / np.minimum(win, t + 1)).astype(np.float32)[None, :]
    return c


_CACHE = {}


def kernel(x_prompt, x_sample, state_conv_a, state_conv_b, state_pool, norm_g, final_norm_g,
           w_ffn_gate, w_ffn_up, w_ffn_down, w_in_conv, conv_a_w, conv_b_w, ln_b_g, ln_b_b,
           w_out_conv, w_pool, pool_scale):
    npass = int(os.environ.get("MK_NPASS", "4"))
    f = lambda a: np.ascontiguousarray(np.asarray(a, dtype=np.float32))
    x_prompt, x_sample = f(x_prompt), f(x_sample)
    state_conv_a, state_conv_b, state_pool = f(state_conv_a), f(state_conv_b), f(state_pool)
    vecs = _pack_vecs(f(norm_g), f(final_norm_g), f(conv_a_w), f(conv_b_w), f(ln_b_g), f(ln_b_b), f(pool_scale))
    cst = _consts()
    shared = {"vecs": vecs, "cst": cst, "wg": f(w_ffn_gate), "wu": f(w_ffn_up), "wd": f(w_ffn_down),
              "win": f(w_in_conv), "wout": f(w_out_conv), "wpool": f(w_pool)}
    if "nc" not in _CACHE or _CACHE.get("npass") != npass:
        _CACHE["nc"], _CACHE["stats"] = build_program(npass)
        _CACHE["npass"] = npass
    nc = _CACHE["nc"]
    in_maps = []
    for k in range(NCORES):
        m = dict(shared)
        m["xp"] = np.ascontiguousarray(x_prompt[2 * k:2 * k + 2])
        m["xs"] = np.ascontiguousarray(x_sample[4 * k:4 * k + 4])
        m["sta"] = np.ascontiguousarray(state_conv_a[:, 4 * k:4 * k + 4])
        m["stb"] = np.ascontiguousarray(state_conv_b[:, 4 * k:4 * k + 4])
        m["stp"] = np.ascontiguousarray(state_pool[:, 4 * k:4 * k + 4])
        in_maps.append(m)
    res = run_bass_kernel_spmd(nc, in_maps, core_ids=list(range(NCORES)))
    R = res.results
    cat = lambda key, ax: np.ascontiguousarray(np.concatenate([np.asarray(r[key], dtype=np.float32) for r in R], axis=ax))
    return (cat("yp", 0), cat("ys", 0), cat("oap", 1), cat("oas", 1), cat("obp", 1), cat("obs", 1),
            cat("opp", 1), cat("ops", 1))
```
